# Optimizing a Trainium2 kernel written in Bass

```python
import jax, jax.numpy as jnp
from jax import lax
import numpy as np

D_MODEL = 1024
BATCH = 4
SEQ = 8192
DEPTH = 1

GRID_W = 64
NA_HEAD_DIM = 64
NA_WIDTH = D_MODEL // 2
NA_HEADS = NA_WIDTH // NA_HEAD_DIM
NA_WIN_H = 8
NA_WIN_W = 16
GLA_HEADS = 4
GLA_KEY_WIDTH = D_MODEL // 4
GLA_VAL_WIDTH = D_MODEL // 2
GLA_DK = GLA_KEY_WIDTH // GLA_HEADS
GLA_DV = GLA_VAL_WIDTH // GLA_HEADS
GLA_GATE_RANK = 16
GLA_GATE_TAU = 16.0
GLA_CHUNK = 64
D_FF = 2816
ALPHA = (2 * DEPTH) ** 0.25
BETA = (8 * DEPTH) ** -0.25
LN_EPS = 1e-5
RMS_EPS = 1e-6
IN_SIZES = (NA_WIDTH, NA_WIDTH, NA_WIDTH,
            GLA_KEY_WIDTH, GLA_KEY_WIDTH,
            GLA_VAL_WIDTH, GLA_VAL_WIDTH,
            GLA_GATE_RANK, GLA_GATE_RANK,
            D_MODEL, D_MODEL)
N_IN = 3 * NA_WIDTH + 2 * GLA_KEY_WIDTH + 2 * GLA_VAL_WIDTH + 2 * GLA_GATE_RANK + 2 * D_MODEL

kernel_name = "hybrid_na_gla_macaron_deepnorm"


def layer_norm(x, g, b):
    xf = x.astype(jnp.float32)
    mu = jnp.mean(xf, axis=-1, keepdims=True)
    var = jnp.mean(jnp.square(xf - mu), axis=-1, keepdims=True)
    y = (xf - mu) * lax.rsqrt(var + LN_EPS)
    return (y * g + b).astype(x.dtype)


def swiglu_ffn(x, w_gate, w_up, w_down):
    return (jax.nn.silu(x @ w_gate) * (x @ w_up)) @ w_down


def split_points():
    pts, acc = [], 0
    for sz in IN_SIZES[:-1]:
        acc += sz
        pts.append(acc)
    return pts


def neighborhood_attention(q, k, v, rpb):
    b, s, h, dh = q.shape
    rows = s // GRID_W
    kh = min(NA_WIN_H, rows)
    kw = NA_WIN_W
    grid = lambda t: t.reshape(b, rows, GRID_W, h, dh).transpose(0, 3, 1, 2, 4)
    qg = grid(q) * (dh ** -0.5)
    kg, vg = grid(k), grid(v)
    cols = jnp.arange(GRID_W)
    col_start = jnp.clip(cols - kw // 2, 0, GRID_W - kw)
    col_idx = col_start[:, None] + jnp.arange(kw)[None, :]
    dc_idx = col_idx - cols[:, None] + (kw - 1)
    rpb_c = rpb[:, :, dc_idx]

    def row_block(args):
        r, q_r = args
        r_start = jnp.clip(r - kh // 2, 0, rows - kh)
        k_band = lax.dynamic_slice_in_dim(kg, r_start, kh, axis=2)
        v_band = lax.dynamic_slice_in_dim(vg, r_start, kh, axis=2)
        k_sel = k_band[:, :, :, col_idx]
        v_sel = v_band[:, :, :, col_idx]
        dr_idx = r_start + jnp.arange(kh) - r + (NA_WIN_H - 1)
        bias = jnp.take(rpb_c, dr_idx, axis=1).transpose(0, 2, 1, 3)
        scores = jnp.einsum('bhqd,bhiqjd->bhqij', q_r, k_sel).astype(jnp.float32)
        scores = scores + bias[None].astype(jnp.float32)
        p = jax.nn.softmax(scores.reshape(b, h, GRID_W, kh * kw), axis=-1)
        p = p.reshape(b, h, GRID_W, kh, kw).astype(v.dtype)
        return jnp.einsum('bhqij,bhiqjd->bhqd', p, v_sel)

    out = lax.map(row_block, (jnp.arange(rows), qg.transpose(2, 0, 1, 3, 4)))
    return out.transpose(1, 0, 3, 2, 4).reshape(b, s, h * dh)


def gla_direction(q, k, v, log_a, include_diag):
    c = q.shape[3]
    b_cum = jnp.cumsum(log_a, axis=3)
    b_last = b_cum[:, :, :, -1:, :]
    q_t = q * jnp.exp(b_cum)
    k_t = k * jnp.exp(-b_cum)
    mask = jnp.tril(jnp.ones((c, c), dtype=bool), k=0 if include_diag else -1)
    attn = jnp.where(mask, jnp.einsum('bhncd,bhnsd->bhncs', q_t, k_t), 0.0)
    o_intra = jnp.einsum('bhncs,bhnse->bhnce', attn, v)
    k_end = k * jnp.exp(b_last - b_cum)
    state_add = jnp.einsum('bhncd,bhnce->bhnde', k_end, v)
    chunk_decay = jnp.exp(b_last[:, :, :, 0, :])

    def step(state, inp):
        dec, add = inp
        return dec[..., None] * state + add, state

    init = jnp.zeros(q.shape[:2] + (q.shape[-1], v.shape[-1]), q.dtype)
    _, states = lax.scan(step, init, (jnp.moveaxis(chunk_decay, 2, 0), jnp.moveaxis(state_add, 2, 0)))
    states = jnp.moveaxis(states, 0, 2)
    o_inter = jnp.einsum('bhncd,bhnde->bhnce', q_t, states)
    return o_intra + o_inter


def gla_bidirectional(q, k, v, g_out, dec_lr_f, dec_lr_b, w_dec2, b_dec, norm_g):
    bsz, s, _ = q.shape
    n = s // GLA_CHUNK
    f32 = jnp.float32

    def chunks(t, d):
        return t.astype(f32).reshape(bsz, n, GLA_CHUNK, GLA_HEADS, d).transpose(0, 3, 1, 2, 4)

    def unchunk(t):
        return t.transpose(0, 2, 3, 1, 4).reshape(bsz, s, GLA_HEADS, GLA_DV)

    qs = q * (GLA_DK ** -0.5)
    log_a_f = jax.nn.log_sigmoid((dec_lr_f @ w_dec2[0] + b_dec[0]).astype(f32)) / GLA_GATE_TAU
    log_a_b = jax.nn.log_sigmoid((dec_lr_b @ w_dec2[1] + b_dec[1]).astype(f32)) / GLA_GATE_TAU
    o_f = gla_direction(chunks(qs, GLA_DK), chunks(k, GLA_DK), chunks(v, GLA_DV),
                        chunks(log_a_f, GLA_DK), True)
    flip = lambda t: jnp.flip(t, axis=1)
    o_b = gla_direction(chunks(flip(qs), GLA_DK), chunks(flip(k), GLA_DK), chunks(flip(v), GLA_DV),
                        chunks(flip(log_a_b), GLA_DK), False)
    o = unchunk(o_f) + jnp.flip(unchunk(o_b), axis=1)
    o = o * lax.rsqrt(jnp.mean(jnp.square(o), axis=-1, keepdims=True) + RMS_EPS) * norm_g
    o = o * jax.nn.silu(g_out.astype(f32).reshape(bsz, s, GLA_HEADS, GLA_DV))
    return o.reshape(bsz, s, GLA_VAL_WIDTH).astype(q.dtype)


def hybrid_mixer(h, w_in, na_rpb, gla_w_dec2, gla_b_dec, gla_norm_g, w_branch_na, w_branch_gla, w_out):
    b, s, _ = h.shape
    proj = h @ w_in
    na_q, na_k, na_v, g_q, g_k, g_v, g_g, dlf, dlb, gate_na, gate_gla = jnp.split(proj, split_points(), axis=-1)
    heads = lambda t: t.reshape(b, s, NA_HEADS, NA_HEAD_DIM)
    y_na = neighborhood_attention(heads(na_q), heads(na_k), heads(na_v), na_rpb) @ w_branch_na
    y_gla = gla_bidirectional(g_q, g_k, g_v, g_g, dlf, dlb, gla_w_dec2, gla_b_dec, gla_norm_g) @ w_branch_gla
    merged = jax.nn.sigmoid(gate_na) * y_na + jax.nn.sigmoid(gate_gla) * y_gla
    return merged @ w_out


def setup_inputs(seed: int = 0) -> dict:
    key = jax.random.key(seed)
    ks = jax.random.split(key, 24)
    L, D, F = DEPTH, D_MODEL, D_FF
    nrm = lambda k, shape, scale: jax.random.normal(k, shape, jnp.float32) * scale
    col_scale = jnp.concatenate([jnp.full((sz,), BETA if i in (2, 5) else 1.0, jnp.float32)
                                 for i, sz in enumerate(IN_SIZES)])
    return {
        "x": nrm(ks[0], (BATCH, SEQ, D), 1.0),
        "ffn1_w_gate": nrm(ks[1], (L, D, F), D ** -0.5),
        "ffn1_w_up": nrm(ks[2], (L, D, F), D ** -0.5),
        "ffn1_w_down": nrm(ks[3], (L, F, D), BETA * F ** -0.5),
        "ln1_g": 1.0 + nrm(ks[4], (L, D), 0.02),
        "ln1_b": nrm(ks[5], (L, D), 0.02),
        "w_in": nrm(ks[6], (L, D, N_IN), D ** -0.5) * col_scale,
        "na_rpb": nrm(ks[7], (L, NA_HEADS, 2 * NA_WIN_H - 1, 2 * NA_WIN_W - 1), 0.02),
        "gla_w_dec2": nrm(ks[8], (L, 2, GLA_GATE_RANK, GLA_KEY_WIDTH), GLA_GATE_RANK ** -0.5),
        "gla_b_dec": nrm(ks[9], (L, 2, GLA_KEY_WIDTH), 0.01),
        "gla_norm_g": 1.0 + nrm(ks[10], (L, GLA_DV), 0.02),
        "w_branch_na": nrm(ks[11], (L, NA_WIDTH, D), NA_WIDTH ** -0.5),
        "w_branch_gla": nrm(ks[12], (L, GLA_VAL_WIDTH, D), GLA_VAL_WIDTH ** -0.5),
        "w_out": nrm(ks[13], (L, D, D), BETA * D ** -0.5),
        "ln2_g": 1.0 + nrm(ks[14], (L, D), 0.02),
        "ln2_b": nrm(ks[15], (L, D), 0.02),
        "ffn2_w_gate": nrm(ks[16], (L, D, F), D ** -0.5),
        "ffn2_w_up": nrm(ks[17], (L, D, F), D ** -0.5),
        "ffn2_w_down": nrm(ks[18], (L, F, D), BETA * F ** -0.5),
        "ln3_g": 1.0 + nrm(ks[19], (L, D), 0.02),
        "ln3_b": nrm(ks[20], (L, D), 0.02),
    }


def reference(x, ffn1_w_gate, ffn1_w_up, ffn1_w_down, ln1_g, ln1_b, w_in, na_rpb, gla_w_dec2,
              gla_b_dec, gla_norm_g, w_branch_na, w_branch_gla, w_out, ln2_g, ln2_b,
              ffn2_w_gate, ffn2_w_up, ffn2_w_down, ln3_g, ln3_b):
    for l in range(DEPTH):
        x = layer_norm(ALPHA * x + 0.5 * swiglu_ffn(x, ffn1_w_gate[l], ffn1_w_up[l], ffn1_w_down[l]),
                       ln1_g[l], ln1_b[l])
        x = layer_norm(ALPHA * x + hybrid_mixer(x, w_in[l], na_rpb[l], gla_w_dec2[l], gla_b_dec[l],
                                                gla_norm_g[l], w_branch_na[l], w_branch_gla[l], w_out[l]),
                       ln2_g[l], ln2_b[l])
        x = layer_norm(ALPHA * x + 0.5 * swiglu_ffn(x, ffn2_w_gate[l], ffn2_w_up[l], ffn2_w_down[l]),
                       ln3_g[l], ln3_b[l])
    return x
```

```python
import numpy as np
from contextlib import ExitStack
import concourse.bass as bass
import concourse.mybir as mybir
from concourse.bass_utils import run_bass_kernel_spmd

F32 = mybir.dt.float32
BF16 = mybir.dt.bfloat16
AF = mybir.ActivationFunctionType
ALU = mybir.AluOpType

D = 1024
DFF = 2816
NFC = DFF // 128
NOWN = 4096
NHALO = 256
NLOC = NOWN + NHALO
ALPHA = 2.0 ** 0.25
LN_EPS = 1e-5
N_IN = 5152


class Buf:
    __slots__ = ("name", "writers", "readers", "sem", "cnt")

    def __init__(self, name):
        self.name = name
        self.writers = set()
        self.readers = set()
        self.sem = None
        self.cnt = 0


class Op:
    __slots__ = ("eng", "fn", "deps", "raw", "needed", "ms", "dma", "idx", "cc")

    def __init__(self, eng, fn):
        self.eng = eng
        self.fn = fn
        self.deps = set()
        self.raw = set()
        self.needed = False
        self.ms = None
        self.dma = None
        self.idx = 0
        self.cc = False


import os
DBG = {k[5:].lower(): int(v) for k, v in os.environ.items() if k.startswith("KDBG_")}
COMPUTE = ("pe", "act", "dve", "pool")
QUEUES = ("pe", "act", "dve", "pool", "sp")


class Prog:
    def __init__(self, nc, es):
        self.nc = nc
        self.es = es
        self.block = None
        self.ops = {q: [] for q in QUEUES}
        self.esem = {q: es.enter_context(nc.semaphore("sem_" + q)) for q in COMPUTE}
        self.ecnt = {q: 0 for q in COMPUTE}
        self.waited = {q: {} for q in QUEUES}
        self.nsem = 0
        self.last = {q: None for q in COMPUTE}
        self.dma_events = []
        self.nops = 0
        psum = es.enter_context(nc.psum_tensor("psum", [128, 8, 512], F32))
        self.psum = psum
        self.pbufs = [Buf("ps%d" % i) for i in range(8)]
        self.pnext = 0
        self.freesems = []
        self.phase_bufs = []
        self.reserved = set()

    def bank(self):
        while self.pnext in self.reserved:
            self.pnext = (self.pnext + 1) % 8
        i = self.pnext
        self.pnext = (i + 1) % 8
        return self.pbufs[i], self.psum[:, i, :]

    def bank2(self):
        if self.pnext % 2:
            self.pnext = (self.pnext + 1) % 8
        while self.pnext in self.reserved:
            self.pnext = (self.pnext + 2) % 8
        i = self.pnext
        self.pnext = (i + 2) % 8
        return (self.pbufs[i], self.pbufs[i + 1]), self.psum[:, i:i + 2, :]

    def bank2_fixed(self, i):
        return (self.pbufs[i], self.pbufs[i + 1]), self.psum[:, i:i + 2, :]

    def newsem(self, name):
        self.nsem += 1
        return self.es.enter_context(self.nc.semaphore("%s_%d" % (name, self.nsem)))

    def bufsem(self, buf):
        if self.freesems:
            buf.sem, buf.cnt = self.freesems.pop()
        else:
            buf.sem, buf.cnt = self.newsem("d"), 0
        self.phase_bufs.append(buf)

    def recycle(self):
        for b in self.phase_bufs:
            self.freesems.append((b.sem, b.cnt))
            b.sem = None
        self.phase_bufs = []

    def _deps(self, o, reads, writes):
        for b in reads:
            o.deps |= b.writers
            o.raw |= b.writers
        for b in writes:
            o.deps |= b.writers
            o.deps |= b.readers
        for b in reads:
            b.readers.add(o)
        for b in writes:
            b.writers = {o}
            b.readers = set()
        o.deps.discard(o)

    def op(self, eng, fn, reads=(), writes=()):
        o = Op(eng, fn)
        self.nops += 1
        o.idx = self.nops
        self._deps(o, reads, writes)
        self.ops[eng].append(o)
        self.last[eng] = o
        return o

    def dma(self, q, out, in_, reads=(), writes=(), sembuf=None, **kw):
        o = Op(q, lambda e: e.dma_start(out=out, in_=in_, **kw))
        self.nops += 1
        o.idx = self.nops
        if sembuf.sem is None:
            self.bufsem(sembuf)
        sembuf.cnt += 16
        o.dma = (sembuf.sem, sembuf.cnt)
        self._deps(o, reads, writes)
        self.ops[q].append(o)
        self.dma_events.append(o)
        return o

    def barrier(self):
        lasts = [self.last[q] for q in COMPUTE if self.last[q] is not None]
        evs = list(self.dma_events)
        for q in QUEUES:
            o = Op(q, None)
            o.deps = set(lasts) | set(evs)
            self.ops[q].append(o)
        self.dma_events = []

    def emit(self):
        for q in QUEUES:
            for o in self.ops[q]:
                for d in o.deps:
                    if d.dma is None:
                        if d.eng == o.eng and (d.eng == "pe" or d not in o.raw):
                            continue
                        d.needed = True
        for q in COMPUTE:
            for o in self.ops[q]:
                if o.needed and o.ms is None and o.dma is None and o.fn is not None:
                    self.ecnt[q] += 1
                    o.ms = self.ecnt[q]
        for q in QUEUES:
            ops = self.ops[q]
            if not ops:
                continue
            self._emit_queue(q, ops)
            self.ops[q] = []
        self.recycle()

    def _emit_queue(self, q, ops):
        waited = self.waited[q]
        esem = self.esem
        starter = {"pe": self.block.tensor, "act": self.block.scalar, "dve": self.block.vector,
                   "pool": self.block.gpsimd, "sp": self.block.sync}[q]

        def body(e):
            for o in ops:
                need = {}
                for d in o.deps:
                    if d.dma is not None:
                        s, v = d.dma
                    else:
                        if d.eng == q and (q == "pe" or d not in o.raw):
                            continue
                        if d.ms is None:
                            continue
                        s, v = esem[d.eng], d.ms
                    k = id(s)
                    if waited.get(k, (None, 0))[1] >= v:
                        continue
                    if k not in need or need[k][1] < v:
                        need[k] = (s, v)
                for k, (s, v) in need.items():
                    e.wait_ge(s, v)
                    waited[k] = (s, v)
                if o.fn is None:
                    continue
                ins = o.fn(e)
                if o.cc:
                    ins.then_inc(o.dma[0])
                elif o.dma is not None:
                    ins.then_inc(o.dma[0], 16)
                elif o.ms is not None:
                    ins.then_inc(esem[q], 1)

        starter(body)

    def mm(self, out, lhsT, rhs, start, stop, reads, writes):
        return self.op("pe", lambda e: e.matmul(out, lhsT=lhsT, rhs=rhs, start=start, stop=stop), reads, writes)

    def tr(self, out, in_, ident, reads, writes):
        return self.op("pe", lambda e: e.transpose(out=out, in_=in_, identity=ident), reads, writes)

    def act(self, out, in_, func, reads, writes, **kw):
        return self.op("act", lambda e: e.activation(out=out, in_=in_, func=func, **kw), reads, writes)

    def tt(self, eng, out, in0, in1, op, reads, writes):
        return self.op(eng, lambda e: e.tensor_tensor(out=out, in0=in0, in1=in1, op=op), reads, writes)

    def ts(self, eng, out, in0, s1, s2, op0, op1, reads, writes):
        if op1 is None:
            return self.op(eng, lambda e: e.tensor_scalar(out=out, in0=in0, scalar1=s1, scalar2=None, op0=op0), reads, writes)
        return self.op(eng, lambda e: e.tensor_scalar(out=out, in0=in0, scalar1=s1, scalar2=s2, op0=op0, op1=op1), reads, writes)

    def stt(self, out, in0, scalar, in1, op0, op1, reads, writes):
        return self.op("dve", lambda e: e.scalar_tensor_tensor(out=out, in0=in0, scalar=scalar, in1=in1, op0=op0, op1=op1),
                       reads, writes)

    def copy(self, eng, out, in_, reads, writes):
        if eng == "act":
            return self.act(out, in_, AF.Copy, reads, writes)
        return self.op(eng, lambda e: e.tensor_copy(out=out, in_=in_), reads, writes)


class Ctx:
    def __init__(self, P, es):
        self.P = P
        self.es = es
        self.n = 0

    def sb(self, name, shape, dt):
        self.P.nsem += 1
        t = self.es.enter_context(self.P.nc.sbuf_tensor("%s_%d" % (name, self.P.nsem), shape, dt))
        return t, Buf(name)


def convert_weights(P, items, engines, q):
    with ExitStack() as es:
        C = Ctx(P, es)
        CW = 2816
        NS = 4
        ins = [C.sb("cvi", [128, CW], F32) for _ in range(NS)]
        outs = [C.sb("cvo", [128, CW], BF16) for _ in range(NS)]
        u = 0
        for (src, dst, dbuf) in items:
            R, Cc = src.shape
            ncs = (Cc + CW - 1) // CW
            cw = Cc // ncs
            assert cw * ncs == Cc
            for rb in range(R // 128):
                for cs in range(ncs):
                    ti, bi = ins[u % NS]
                    to, bo = outs[u % NS]
                    eng = engines[u % len(engines)]
                    s_ap = src[rb * 128:(rb + 1) * 128, cs * cw:(cs + 1) * cw]
                    d_ap = dst[rb * 128:(rb + 1) * 128, cs * cw:(cs + 1) * cw]
                    P.dma(q, ti[:, 0:cw], s_ap, writes=[bi], sembuf=bi)
                    if eng == "act":
                        P.op("act", lambda e, to=to, ti=ti, cw=cw: e.activation(out=to[:, 0:cw], in_=ti[:, 0:cw], func=AF.Copy),
                             reads=[bi], writes=[bo])
                    else:
                        P.op(eng, lambda e, to=to, ti=ti, cw=cw: e.tensor_copy(out=to[:, 0:cw], in_=ti[:, 0:cw]),
                             reads=[bi], writes=[bo])
                    o = P.dma("pool", d_ap, to[:, 0:cw], reads=[bo], sembuf=bo)
                    dbuf.writers.add(o)
                    u += 1
        P.barrier()
        P.emit()


class BgConv:
    def __init__(self, P, C, items, cw=1408):
        self.P = P
        self.units = []
        for (src, dst) in items:
            R, Cc = src.shape
            ncs = (Cc + cw - 1) // cw
            w = Cc // ncs
            assert w * ncs == Cc
            for rb in range(R // 128):
                for cs in range(ncs):
                    self.units.append((src[rb * 128:(rb + 1) * 128, cs * w:(cs + 1) * w],
                                       dst[rb * 128:(rb + 1) * 128, cs * w:(cs + 1) * w], w))
        self.ins = [C.sb("bgi", [128, cw], F32) for _ in range(3)]
        self.outs = [C.sb("bgo", [128, cw], BF16) for _ in range(3)]
        self.u = 0

    def emit(self, n):
        P = self.P
        for _ in range(n):
            if self.u >= len(self.units):
                return
            s_ap, d_ap, w = self.units[self.u]
            ti, bi = self.ins[self.u % 3]
            to, bo = self.outs[self.u % 3]
            P.dma("sp", ti[:, 0:w], s_ap, writes=[bi], sembuf=bi)
            P.copy("pool", to[:, 0:w], ti[:, 0:w], [bi], [bo])
            P.dma("pool", d_ap, to[:, 0:w], reads=[bo], sembuf=bo)
            self.u += 1

    def flush(self):
        self.emit(len(self.units))

def layer_norm_rows(P, C, rt, rb, s, G, Bt, cpool):
    stats, sbuf_ = C.sb("lnst", [128, 2, 6], F32)
    mv, mvb = C.sb("lnmv", [128, 2], F32)
    sc, scb = C.sb("lnsc", [128, 4], F32)
    nh, nhb = cpool["neghalf"]
    for h in range(2):
        P.op("dve", lambda e, h=h: e.bn_stats(out=stats[:, h, :], in_=rt[:, s, h * 512:(h + 1) * 512]),
             reads=[rb], writes=[sbuf_])
    P.op("dve", lambda e: e.bn_aggr(out=mv[:, :], in_=stats[:, :, :].rearrange("p a b -> p (a b)")),
         reads=[sbuf_], writes=[mvb])
    P.op("pool", lambda e: e.tensor_scalar(out=sc[:, 0:1], in0=mv[:, 1:2], scalar1=LN_EPS, scalar2=None, op0=ALU.add),
         reads=[mvb], writes=[scb])
    P.op("pool", lambda e: e.tensor_tensor(out=sc[:, 1:2], in0=sc[:, 0:1], in1=nh[:, 0:1], op=ALU.pow),
         reads=[scb, nhb], writes=[scb])
    P.op("pool", lambda e: e.tensor_scalar(out=sc[:, 2:3], in0=mv[:, 0:1], scalar1=-1.0, scalar2=sc[:, 1:2],
                                           op0=ALU.mult, op1=ALU.mult),
         reads=[mvb, scb], writes=[scb])
    P.op("act", lambda e: e.activation(out=rt[:, s, :], in_=rt[:, s, :], func=AF.Identity,
                                       bias=sc[:, 2:3], scale=sc[:, 1:2]),
         reads=[rb, scb], writes=[rb])
    P.op("pool", lambda e: e.tensor_tensor(out=rt[:, s, :], in0=rt[:, s, :], in1=G[0][:, :], op=ALU.mult),
         reads=[rb, G[1]], writes=[rb])
    P.op("pool", lambda e: e.tensor_tensor(out=rt[:, s, :], in0=rt[:, s, :], in1=Bt[0][:, :], op=ALU.add),
         reads=[rb, Bt[1]], writes=[rb])


def load_consts(P, C, q="sp"):
    cp = {}
    nh, nhb = C.sb("neghalf", [128, 1], F32)
    P.op("pool", lambda e: e.memset(nh[:, :], -0.5), writes=[nhb])
    cp["neghalf"] = (nh, nhb)
    return cp


def transpose_rows(P, src, sb_, ns, dstT, dstb, identf, col0=0):
    for k in range(8):
        pb, ps = P.bank()
        for s in range(ns):
            P.op("pe", lambda e, k=k, s=s, ps=ps: e.transpose(out=ps[:, s * 128:(s + 1) * 128],
                                                             in_=src[:, s, k * 128:(k + 1) * 128],
                                                             identity=identf[0][:, :]),
                 reads=[sb_, identf[1]], writes=[pb])
        T = ns * 128
        if k % 2 == 0:
            P.op("act", lambda e, k=k, ps=ps, T=T: e.activation(out=dstT[:, k, col0:col0 + T], in_=ps[:, 0:T], func=AF.Copy),
                 reads=[pb], writes=[dstb])
        else:
            P.op("dve", lambda e, k=k, ps=ps, T=T: e.tensor_copy(out=dstT[:, k, col0:col0 + T], in_=ps[:, 0:T]),
                 reads=[pb], writes=[dstb])


def ffn_phase(P, src, dst, ntok, wg, wu, wd, wbufs, lng, lnb, bg_items=None):
    nc = P.nc
    tiles = []
    t0 = 0
    while t0 < ntok:
        T = min(512, ntok - t0)
        tiles.append((t0, T))
        t0 += T
    SL = 2
    NSL = NFC // SL
    NSLOT = 3
    with ExitStack() as es:
        C = Ctx(P, es)
        cp = load_consts(P, C)
        identf = C.sb("identf", [128, 128], F32)
        P.dma("sp", identf[0][:, :], wbufs["ident"], writes=[identf[1]], sembuf=identf[1])
        G = C.sb("lnG", [128, D], F32)
        Bt = C.sb("lnB", [128, D], F32)
        P.dma("sp", G[0][:, :], lng.partition_broadcast(128), writes=[G[1]], sembuf=G[1])
        P.dma("sp", Bt[0][:, :], lnb.partition_broadcast(128), writes=[Bt[1]], sembuf=Bt[1])
        xts = [C.sb("xt", [128, 4, D], F32) for _ in range(2)]
        rts = [C.sb("rt", [128, 4, D], F32) for _ in range(1)]
        xTs = [C.sb("xT", [128, 8, 512], BF16) for _ in range(2)]
        hTs = [C.sb("hT", [128, NFC, 512], BF16) for _ in range(1)]
        wgus = [C.sb("wgu", [128, 2, 8, SL * 128], BF16) for _ in range(NSLOT)]
        wds = [C.sb("wd", [128, NFC, 512], BF16) for _ in range(2)]
        sgs = [C.sb("sg", [128, 512], F32) for _ in range(2)]
        bgc = BgConv(P, C, bg_items) if bg_items else None

        wgv = wg.rearrange("(kc ki) f -> ki kc f", ki=128)
        wuv = wu.rearrange("(kc ki) f -> ki kc f", ki=128)
        wdv = wd.rearrange("(fc fi) d -> fi fc d", fi=128)
        wdep = [wbufs["wg"], wbufs["wu"]]

        slab_ctr = [0]

        def ld_x(m):
            t0, T = tiles[m]
            ns = T // 128
            xt, xb = xts[m % 2]
            P.dma("sp", xt[:, 0:ns, :], src[t0:t0 + T, :].rearrange("(s p) d -> p s d", p=128),
                  writes=[xb], sembuf=xb)

        def ld_slab(g):
            i = g % NSL
            w, b = wgus[g % NSLOT]
            P.dma("sp", w[:, 0, :, :], wgv[:, :, i * SL * 128:(i + 1) * SL * 128], reads=[wdep[0]], writes=[b], sembuf=b)
            P.dma("sp", w[:, 1, :, :], wuv[:, :, i * SL * 128:(i + 1) * SL * 128], reads=[wdep[1]], writes=[b], sembuf=b)

        def ld_wd(h):
            w, b = wds[h]
            P.dma("sp", w[:, :, :], wdv[:, :, h * 512:(h + 1) * 512], reads=[wbufs["wd"]], writes=[b], sembuf=b)

        ntl = len(tiles)
        ld_x(0)
        for g in range(NSLOT):
            ld_slab(g)
        ld_wd(0)
        ld_wd(1)
        out_events = []
        for m in range(ntl):
            t0, T = tiles[m]
            ns = T // 128
            xt, xb = xts[m % 2]
            rt, rb = rts[0]
            xT, xTb = xTs[m % 2]
            hT, hb = hTs[0]
            transpose_rows(P, xt, xb, ns, xT, xTb, identf)
            P.op("pool", lambda e, xt=xt, ns=ns: e.tensor_scalar(out=xt[:, 0:ns, :], in0=xt[:, 0:ns, :], scalar1=ALPHA,
                                                              scalar2=0.0, op0=ALU.mult, op1=ALU.add),
                 reads=[xb], writes=[xb])
            for i in range(NSL):
                g = m * NSL + i
                w, wb = wgus[g % NSLOT]
                for j in range(SL):
                    f = i * SL + j
                    pg, psg = P.bank()
                    pu, psu = P.bank()
                    for k in range(8):
                        P.op("pe", lambda e, k=k, j=j, w=w, psg=psg, xT=xT, T=T: e.matmul(
                            psg[:, 0:T], lhsT=w[:, 0, k, j * 128:(j + 1) * 128], rhs=xT[:, k, 0:T],
                            start=(k == 0), stop=(k == 7)), reads=[wb, xTb], writes=[pg])
                    for k in range(8):
                        P.op("pe", lambda e, k=k, j=j, w=w, psu=psu, xT=xT, T=T: e.matmul(
                            psu[:, 0:T], lhsT=w[:, 1, k, j * 128:(j + 1) * 128], rhs=xT[:, k, 0:T],
                            start=(k == 0), stop=(k == 7)), reads=[wb, xTb], writes=[pu])
                    sg, sgb = sgs[f % 2]
                    P.op("act", lambda e, sg=sg, psg=psg, T=T: e.activation(out=sg[:, 0:T], in_=psg[:, 0:T], func=AF.Silu),
                         reads=[pg], writes=[sgb])
                    P.op("dve", lambda e, f=f, sg=sg, psu=psu, hT=hT, T=T: e.tensor_tensor(
                        out=hT[:, f, 0:T], in0=psu[:, 0:T], in1=sg[:, 0:T], op=ALU.mult),
                        reads=[pu, sgb], writes=[hb])
                gn = g + NSLOT
                if gn < ntl * NSL:
                    ld_slab(gn)
            if m + 1 < ntl:
                ld_x(m + 1)
            if bgc is not None:
                bgc.emit((len(bgc.units) + ntl - 1) // ntl)
            for h in range(2):
                wdt, wdb = wds[h]
                for s in range(ns):
                    pb, ps = P.bank()
                    for f in range(NFC):
                        P.op("pe", lambda e, f=f, s=s, ps=ps, hT=hT, wdt=wdt: e.matmul(
                            ps[:, :], lhsT=hT[:, f, s * 128:(s + 1) * 128], rhs=wdt[:, f, :],
                            start=(f == 0), stop=(f == NFC - 1)), reads=[hb, wdb], writes=[pb])
                    P.op("dve", lambda e, s=s, h=h, ps=ps, rt=rt, xt=xt: e.scalar_tensor_tensor(
                        out=rt[:, s, h * 512:(h + 1) * 512], in0=ps[:, :], scalar=0.5,
                        in1=xt[:, s, h * 512:(h + 1) * 512], op0=ALU.mult, op1=ALU.add),
                        reads=[pb, xb], writes=[rb])
                if m + 1 < ntl:
                    ld_wd(h)
            for s in range(ns):
                layer_norm_rows(P, C, rt, rb, s, G, Bt, cp)
            o = P.dma("sp", dst[t0:t0 + T, :].rearrange("(s p) d -> p s d", p=128), rt[:, 0:ns, :],
                      reads=[rb], sembuf=rb)
            out_events.append(o)
        if bgc is not None:
            bgc.flush()
        P.barrier()
        P.emit()
    return out_events


class Gla:
    def __init__(self, P, C, prm, identb):
        self.P = P
        self.C = C
        self.identb = identb
        nc = P.nc
        self.ones = C.sb("g_ones", [128, 128], F32)
        P.op("pool", lambda e: e.memset(self.ones[0][:, :], 1.0), writes=[self.ones[1]])
        self.maskT = C.sb("g_mask", [128, 2, 128], F32)
        P.dma("sp", self.maskT[0][:, :, :], prm["gla_mask"], writes=[self.maskT[1]], sembuf=self.maskT[1])
        wd32 = C.sb("g_wd32", [16, 2, 256], F32)
        P.dma("sp", wd32[0][:, :, :], prm["wdec"], writes=[wd32[1]], sembuf=wd32[1])
        self.wdec = C.sb("g_wdec", [16, 2, 256], BF16)
        P.copy("dve", self.wdec[0][:, :, :], wd32[0][:, :, :], [wd32[1]], [self.wdec[1]])
        self.negb = C.sb("g_negb", [128, 4], F32)
        P.dma("sp", self.negb[0][:, :], prm["bdec"], writes=[self.negb[1]], sembuf=self.negb[1])
        P.ts("dve", self.negb[0][:, :], self.negb[0][:, :], -1.0, None, ALU.mult, None, [self.negb[1]], [self.negb[1]])
        self.S = C.sb("g_S", [128, 2, 128], F32)
        self.Sbf = C.sb("g_Sbf", [128, 2, 128], BF16)
        self.e1 = C.sb("g_e1", [128, 2, 512], F32)
        self.la = C.sb("g_la", [128, 2, 512], F32)
        self.cum = C.sb("g_cum", [128, 2, 512], F32)
        self.eb = C.sb("g_eb", [128, 2, 512], F32)
        self.enb = C.sb("g_enb", [128, 2, 512], F32)
        self.qtT = [(C.sb("g_qtTe", [128, 2, 128], BF16), C.sb("g_qtTo", [128, 2, 128], BF16)) for _ in range(2)]
        for pair in self.qtT:
            for t, b in pair:
                P.op("pool", lambda e, t=t: e.memset(t[:, :, :], 0.0), writes=[b])
        self.ktT = [C.sb("g_ktT", [128, 2, 128], BF16) for _ in range(2)]
        self.kt = [C.sb("g_kt", [128, 2, 128], BF16) for _ in range(2)]
        self.At = [C.sb("g_At", [128, 4, 128], BF16) for _ in range(2)]
        self.tmp = [C.sb("g_tmp", [128, 128], F32) for _ in range(2)]
        self.n = 0

    def zero_state(self):
        P = self.P
        P.op("pool", lambda e: e.memset(self.S[0][:, :, :], 0.0), writes=[self.S[1]])
        P.op("pool", lambda e: e.memset(self.Sbf[0][:, :, :], 0.0), writes=[self.Sbf[1]])

    def macro(self, d, qT, kT, gv, dl, T, chunks, out_cb):
        P = self.P
        sgn = -1.0 / 16.0 if d == 0 else 1.0 / 16.0
        e1, e1b = self.e1
        la, lab = self.la
        cum, cumb = self.cum
        eb, ebb = self.eb
        enb, enbb = self.enb
        for c in range(2):
            pb, ps = P.bank()
            P.mm(ps[:, 0:T], self.wdec[0][0:16, d, c * 128:(c + 1) * 128], dl[0][0:16, 0:T], True, True,
                 [self.wdec[1], dl[1]], [pb])
            P.act(e1[:, c, 0:T], ps[:, 0:T], AF.Exp, [pb, self.negb[1]], [e1b],
                  bias=self.negb[0][:, d * 2 + c:d * 2 + c + 1], scale=-1.0)
            P.act(la[:, c, 0:T], e1[:, c, 0:T], AF.Ln, [e1b], [lab], bias=1.0, scale=1.0)
        LV = DBG.get("gla", 9)
        if LV < 2:
            return
        for s in chunks:
            cs = slice(s * 128, (s + 1) * 128)
            i = self.n % 2
            self.n += 1
            qtm = self.qtT[i]
            ktT, ktb = self.ktT[i]
            kt, ktkb = self.kt[i]
            At, Atb = self.At[i]
            for c in range(2):
                P.op("dve", lambda e, c=c, cs=cs: e.tensor_tensor_scan(
                    out=cum[:, c, cs], data0=self.ones[0][:, :], data1=la[:, c, cs], initial=0.0,
                    op0=ALU.mult, op1=ALU.add), reads=[self.ones[1], lab], writes=[cumb])
                if LV < 3:
                    continue
                cx, cxb = cum, cumb
                if d == 1:
                    cx, cxb = e1, e1b
                    P.stt(cx[:, c, cs], cum[:, c, cs], cum[:, c, s * 128 + 127:s * 128 + 128], la[:, c, cs],
                          ALU.subtract, ALU.subtract, [cumb, lab], [cxb])
                P.act(eb[:, c, cs], cx[:, c, cs], AF.Exp, [cxb], [ebb], scale=sgn)
                P.act(enb[:, c, cs], cx[:, c, cs], AF.Exp, [cxb], [enbb], scale=-sgn)
                for hh in range(2):
                    r0 = hh * 64
                    P.stt(qtm[hh][0][r0:r0 + 64, c, :], qT[0][r0:r0 + 64, c, cs], 0.125, eb[r0:r0 + 64, c, cs],
                          ALU.mult, ALU.mult, [qT[1], ebb], [qtm[hh][1]])
                P.tt("pool", ktT[:, c, :], kT[0][:, c, cs], enb[:, c, cs], ALU.mult, [kT[1], enbb], [ktb])
                if LV < 4:
                    continue
                pb, ps = P.bank()
                psb = ps.bitcast(BF16)
                P.tr(psb[:, 0:128], ktT[:, c, :], self.identb[0][:, :], [ktb, self.identb[1]], [pb])
                P.copy("act", kt[:, c, :], psb[:, 0:128], [pb], [ktkb])
            if LV < 5:
                continue
            pa, psa = P.bank()
            for h in range(4):
                c, hh = h // 2, h % 2
                P.mm(psa[:, h * 128:(h + 1) * 128], ktT[:, c, :], qtm[hh][0][:, c, :], True, True,
                     [ktb, qtm[hh][1]], [pa])
            for h in range(4):
                P.tt("dve", At[:, h, :], psa[:, h * 128:(h + 1) * 128], self.maskT[0][:, d, :], ALU.mult,
                     [pa, self.maskT[1]], [Atb])
            if LV < 6:
                continue
            sas = []
            for c in range(2):
                pb, ps = P.bank()
                for hh in range(2):
                    h = 2 * c + hh
                    P.mm(ps[:, hh * 128:(hh + 1) * 128], kt[:, c, :], gv[0][:, s, h * 128:(h + 1) * 128], True, True,
                         [ktkb, gv[1]], [pb])
                sas.append((pb, ps))
            if LV < 7:
                continue
            po, pso = P.bank()
            for h in range(4):
                c, hh = h // 2, h % 2
                P.mm(pso[:, h * 128:(h + 1) * 128], At[:, h, :], gv[0][:, s, h * 128:(h + 1) * 128], True, False,
                     [Atb, gv[1]], [po])
                P.mm(pso[:, h * 128:(h + 1) * 128], qtm[hh][0][:, c, :], self.Sbf[0][:, c, :], False, True,
                     [qtm[hh][1], self.Sbf[1]], [po])
            out_cb(s, po, pso)
            if LV < 8:
                continue
            dcol = s * 128 + 127 if d == 0 else s * 128
            for c in range(2):
                pb, ps = sas[c]
                tmp, tmpb = self.tmp[c]
                for hh in range(2):
                    r0 = hh * 64
                    P.tt("dve", tmp[r0:r0 + 64, :], ps[r0:r0 + 64, hh * 128:(hh + 1) * 128], self.S[0][r0:r0 + 64, c, :],
                         ALU.add, [pb, self.S[1]], [tmpb])
                P.act(self.S[0][:, c, :], tmp[:, :], AF.Copy, [tmpb, ebb], [self.S[1]], scale=eb[:, c, dcol:dcol + 1])
                P.copy("pool", self.Sbf[0][:, c, :], self.S[0][:, c, :], [self.S[1]], [self.Sbf[1]])


COLS = {"na_q": 0, "na_k": 512, "na_v": 1024, "g_q": 1536, "g_k": 1792, "g_v": 2048, "g_g": 2560,
        "dl0": 3072, "dl1": 3088, "gate_na": 3104, "gate_gla": 4128}


def proj_gla_phase(P, X1, Win, scr, prm, ident_in):
    nc = P.nc
    tiles = [(m * 512, 512) for m in range(8)] + [(4096, 256)]
    with ExitStack() as es:
        C = Ctx(P, es)
        identf = C.sb("identf", [128, 128], F32)
        P.dma("sp", identf[0][:, :], ident_in, writes=[identf[1]], sembuf=identf[1])
        identb = C.sb("identb", [128, 128], BF16)
        P.copy("dve", identb[0][:, :], identf[0][:, :], [identf[1]], [identb[1]])
        G = Gla(P, C, prm, identb)
        G.zero_state()
        x1ts = [C.sb("x1t", [128, 4, D], F32) for _ in range(2)]
        x1Ts = [C.sb("x1T", [128, 8, 512], BF16) for _ in range(2)]
        wsl = [C.sb("wsl", [128, 8, 512], BF16) for _ in range(3)]
        stg32 = [C.sb("stg32", [128, 512], F32) for _ in range(4)]
        stg16 = [C.sb("stg16", [128, 512], BF16) for _ in range(4)]
        vaug = [C.sb("vaug", [128, 8, 65], BF16) for _ in range(2)]
        for v, vb in vaug:
            P.op("pool", lambda e, v=v: e.memset(v[:, :, :], 1.0), writes=[vb])
        gqTs = [C.sb("gqT", [128, 2, 512], F32) for _ in range(2)]
        gkTs = [C.sb("gkT", [128, 2, 512], F32) for _ in range(2)]
        gvs = [C.sb("gv", [128, 4, 512], BF16) for _ in range(2)]
        dl0s = [C.sb("dl0", [16, 512], BF16) for _ in range(2)]
        dl1s = [C.sb("dl1", [16, 512], BF16) for _ in range(2)]
        ofs = [C.sb("ofs", [128, 512], F32) for _ in range(2)]
        Wv = Win.rearrange("(kc ki) n -> ki kc n", ki=128)
        cnt = {"w": 0, "s32": 0, "s16": 0, "ev": 0, "of": 0}

        def ldw(c0, ncol):
            w, wb = wsl[cnt["w"] % 3]
            cnt["w"] += 1
            P.dma("sp", w[:, :, 0:ncol], Wv[:, :, c0:c0 + ncol], writes=[wb], sembuf=wb)
            return w, wb

        def evac_eng():
            cnt["ev"] += 1
            return "act" if cnt["ev"] % 2 else "dve"

        def ld_x(m):
            t0, T = tiles[m]
            ns = T // 128
            xt, xb = x1ts[m % 2]
            P.dma("sp", xt[:, 0:ns, :], X1[t0:t0 + T, :].rearrange("(s p) d -> p s d", p=128), writes=[xb], sembuf=xb)

        def fm_group(w, wb, xT, xTb, T, lc, M=128):
            pb, ps = P.bank()
            for k in range(8):
                P.mm(ps[0:M, 0:T], w[:, k, lc:lc + M], xT[:, k, 0:T], k == 0, k == 7, [wb, xTb], [pb])
            return pb, ps

        def tm_group(w, wb, xT, xTb, s):
            pb, ps = P.bank()
            for k in range(8):
                P.mm(ps[:, :], xT[:, k, s * 128:(s + 1) * 128], w[:, k, 0:512], k == 0, k == 7, [wb, xTb], [pb])
            return pb, ps

        def store_fm(dst, r0, t0, T, pb, ps, dt):
            if dt == BF16:
                st, sb_ = stg16[cnt["s16"] % 4]
                cnt["s16"] += 1
            else:
                st, sb_ = stg32[cnt["s32"] % 4]
                cnt["s32"] += 1
            P.copy(evac_eng(), st[:, 0:T], ps[:, 0:T], [pb], [sb_])
            P.dma("sp", dst[r0:r0 + 128, t0:t0 + T], st[:, 0:T], reads=[sb_], sembuf=sb_)

        ld_x(0)
        for m in range(len(tiles)):
            t0, T = tiles[m]
            ns = T // 128
            halo = m == 8
            xt, xb = x1ts[m % 2]
            xT, xTb = x1Ts[m % 2]
            transpose_rows(P, xt, xb, ns, xT, xTb, identf)
            if m + 1 < len(tiles):
                ld_x(m + 1)
            if not halo:
                w, wb = ldw(COLS["na_q"], 512)
                for cc in range(4):
                    pb, ps = fm_group(w, wb, xT, xTb, T, cc * 128)
                    store_fm(scr["NAQT"], cc * 128, t0, T, pb, ps, BF16)
            w, wb = ldw(COLS["na_k"], 512)
            for cc in range(4):
                pb, ps = fm_group(w, wb, xT, xTb, T, cc * 128)
                store_fm(scr["NAKT"], cc * 128, t0, T, pb, ps, BF16)
            w, wb = ldw(COLS["na_v"], 512)
            for s in range(ns):
                pb, ps = tm_group(w, wb, xT, xTb, s)
                va, vab = vaug[s % 2]
                P.copy(evac_eng(), va[:, :, 0:64], ps[:, :].rearrange("p (h d) -> p h d", d=64), [pb], [vab])
                P.dma("sp", scr["NAV"][t0 + s * 128:t0 + (s + 1) * 128, :], va[:, :, :].rearrange("p h d -> p (h d)"),
                      reads=[vab], sembuf=vab)
            if halo:
                continue
            gqT = gqTs[m % 2]
            gkT = gkTs[m % 2]
            w, wb = ldw(COLS["g_q"], 512)
            for cc in range(4):
                pb, ps = fm_group(w, wb, xT, xTb, T, cc * 128)
                tgt = gqT if cc < 2 else gkT
                P.copy(evac_eng(), tgt[0][:, cc % 2, 0:T], ps[:, 0:T], [pb], [tgt[1]])
                P.dma("sp", scr["GQT" if cc < 2 else "GKT"][(cc % 2) * 128:(cc % 2 + 1) * 128, t0:t0 + T],
                      tgt[0][:, cc % 2, 0:T], reads=[tgt[1]], sembuf=tgt[1])
            gv = gvs[m % 2]
            w, wb = ldw(COLS["g_v"], 512)
            for s in range(ns):
                pb, ps = tm_group(w, wb, xT, xTb, s)
                P.copy(evac_eng(), gv[0][:, s, :], ps[:, :], [pb], [gv[1]])
            P.dma("sp", scr["GV"][t0:t0 + T, :].rearrange("(s p) d -> p s d", p=128), gv[0][:, 0:ns, :],
                  reads=[gv[1]], sembuf=gv[1])
            w, wb = ldw(COLS["g_g"], 512)
            for s in range(ns):
                pb, ps = tm_group(w, wb, xT, xTb, s)
                st, sb_ = stg32[cnt["s32"] % 4]
                cnt["s32"] += 1
                P.copy(evac_eng(), st[:, :], ps[:, :], [pb], [sb_])
                P.dma("sp", scr["GG"][t0 + s * 128:t0 + (s + 1) * 128, :], st[:, :], reads=[sb_], sembuf=sb_)
            w, wb = ldw(COLS["dl0"], 32)
            dl0 = dl0s[m % 2]
            dl1 = dl1s[m % 2]
            for dd, dl in ((0, dl0), (1, dl1)):
                pb, ps = fm_group(w, wb, xT, xTb, T, dd * 16, M=16)
                P.copy(evac_eng(), dl[0][:, 0:T], ps[0:16, 0:T], [pb], [dl[1]])
            P.dma("sp", scr["DLB"][:, t0:t0 + T], dl1[0][:, 0:T], reads=[dl1[1]], sembuf=dl1[1])
            for nm, dst in (("gate_na", "GNA"), ("gate_gla", "GGL")):
                for hf in range(2):
                    w, wb = ldw(COLS[nm] + hf * 512, 512)
                    for cc in range(4):
                        pb, ps = fm_group(w, wb, xT, xTb, T, cc * 128)
                        store_fm(scr[dst], (hf * 4 + cc) * 128, t0, T, pb, ps, F32)

            def out_cb(s, po, pso, t0=t0):
                of, ofb = ofs[cnt["of"] % 2]
                cnt["of"] += 1
                P.copy(evac_eng(), of[:, :], pso[:, :], [po], [ofb])
                P.dma("sp", scr["OF"][t0 + s * 128:t0 + (s + 1) * 128, :], of[:, :], reads=[ofb], sembuf=ofb)

            if DBG.get("gla", 9) > 0:
                G.macro(0, gqT, gkT, gv, dl0, T, list(range(ns)), out_cb)
        P.dma("sp", scr["SOUT"].rearrange("(c p) e -> p c e", p=128), G.S[0][:, :, :], reads=[G.S[1]], sembuf=G.S[1])
        P.barrier()
        P.emit()


def select_state_phase(P, scr, prm):
    with ExitStack() as es:
        C = Ctx(P, es)
        sall = C.sb("sall", [128, 8, 256], F32)
        sel = C.sb("sel", [128, 8], F32)
        acc = C.sb("sacc", [128, 256], F32)
        P.dma("sp", sall[0][:, :, :].rearrange("p r (c e) -> p r c e", c=2),
              scr["SALL"].rearrange("(r c p) e -> p r c e", c=2, p=128), writes=[sall[1]], sembuf=sall[1])
        P.dma("sp", sel[0][:, :], prm["sel"].partition_broadcast(128), writes=[sel[1]], sembuf=sel[1])
        P.ts("dve", acc[0][:, :], sall[0][:, 0, :], sel[0][:, 0:1], None, ALU.mult, None, [sall[1], sel[1]], [acc[1]])
        for r in range(1, 8):
            P.stt(acc[0][:, :], sall[0][:, r, :], sel[0][:, r:r + 1], acc[0][:, :], ALU.mult, ALU.add,
                  [sall[1], sel[1], acc[1]], [acc[1]])
        P.dma("sp", scr["SIN"].rearrange("(c p) e -> p c e", p=128), acc[0][:, :].rearrange("p (c e) -> p c e", c=2),
              reads=[acc[1]], sembuf=acc[1])
        P.barrier()
        P.emit()


def attn_phase(P, X1, X2, scr, prm, Wb, lng, lnb, ident_in):
    nc = P.nc
    with ExitStack() as es:
        C = Ctx(P, es)
        cp = load_consts(P, C)
        identf = C.sb("identf", [128, 128], F32)
        P.dma("sp", identf[0][:, :], ident_in, writes=[identf[1]], sembuf=identf[1])
        identb = C.sb("identb", [128, 128], BF16)
        P.copy("dve", identb[0][:, :], identf[0][:, :], [identf[1]], [identb[1]])
        Gt = C.sb("lnG", [128, D], F32)
        Bt = C.sb("lnB", [128, D], F32)
        P.dma("sp", Gt[0][:, :], lng.partition_broadcast(128), writes=[Gt[1]], sembuf=Gt[1])
        P.dma("sp", Bt[0][:, :], lnb.partition_broadcast(128), writes=[Bt[1]], sembuf=Bt[1])
        NG = C.sb("normg", [128, 128], F32)
        P.dma("sp", NG[0][:, :], prm["norm_g"].partition_broadcast(128), writes=[NG[1]], sembuf=NG[1])
        nh4 = C.sb("nh4", [128, 4], F32)
        P.op("pool", lambda e: e.memset(nh4[0][:, :], -0.5), writes=[nh4[1]])
        G = Gla(P, C, prm, identb)
        P.dma("sp", G.S[0][:, :, :], scr["SIN"].rearrange("(c p) e -> p c e", p=128), writes=[G.S[1]], sembuf=G.S[1])
        P.copy("pool", G.Sbf[0][:, :, :], G.S[0][:, :, :], [G.S[1]], [G.Sbf[1]])
        tbl = C.sb("natbl", [128, 8, 640], F32)
        P.dma("sp", tbl[0][:, :, :], prm["na_bias"][2], writes=[tbl[1]], sembuf=tbl[1])
        wbna = C.sb("wbna", [128, 4, D], BF16)
        wbgl = C.sb("wbgl", [128, 4, D], BF16)
        wout = C.sb("wout", [128, 8, D], BF16)
        P.dma("sp", wbna[0][:, :, :], Wb["w_branch_na"].rearrange("(kc ki) n -> ki kc n", ki=128), writes=[wbna[1]], sembuf=wbna[1])
        P.dma("sp", wbgl[0][:, :, :], Wb["w_branch_gla"].rearrange("(kc ki) n -> ki kc n", ki=128), writes=[wbgl[1]], sembuf=wbgl[1])
        P.dma("sp", wout[0][:, :, :], Wb["w_out"].rearrange("(kc ki) n -> ki kc n", ki=128), writes=[wout[1]], sembuf=wout[1])
        x1t = C.sb("x1t", [128, 4, D], F32)
        qTm = [C.sb("naqTe", [128, 4, 512], BF16), C.sb("naqTo", [128, 4, 512], BF16)]
        for t, b in qTm:
            P.op("pool", lambda e, t=t: e.memset(t[:, :, :], 0.0), writes=[b])
        kring = C.sb("kring", [128, 4, 7, 128], BF16)
        vring = C.sb("vring", [128, 7, 520], BF16)
        kb = [Buf("kr%d" % i) for i in range(7)]
        vb = [Buf("vr%d" % i) for i in range(7)]
        Tts = [C.sb("Tt", [128, 640], F32) for _ in range(2)]
        Pts = [C.sb("Pt", [128, 5, 128], BF16) for _ in range(2)]
        onas = [C.sb("ona", [128, 512], BF16) for _ in range(2)]
        recs = [C.sb("rec", [128, 8], F32) for _ in range(2)]
        gqT = C.sb("gqT", [128, 2, 512], F32)
        gkT = C.sb("gkT", [128, 2, 512], F32)
        gv = C.sb("gv", [128, 4, 512], BF16)
        dl1 = C.sb("dl1", [16, 512], BF16)
        ggs = [C.sb("gg", [128, 512], F32) for _ in range(2)]
        ofs = [C.sb("of", [128, 512], F32) for _ in range(2)]
        osums = [C.sb("osum", [128, 512], F32) for _ in range(2)]
        ogls = [C.sb("ogl", [128, 512], BF16) for _ in range(2)]
        junk = C.sb("junk", [128, 128], F32)
        ssqs = [C.sb("ssq", [128, 4], F32) for _ in range(2)]
        onaT = C.sb("onaT", [128, 4, 512], BF16)
        oglT = C.sb("oglT", [128, 4, 512], BF16)
        mT = C.sb("mT", [128, 8, 512], BF16)
        gnas = [C.sb("gna", [128, 512], F32) for _ in range(2)]
        ggls = [C.sb("ggl", [128, 512], F32) for _ in range(2)]
        NAKv = scr["NAKT"].rearrange("(c p) t -> p c t", p=128)
        cnt = {"i": 0, "g": 0, "ev": 0}

        def ld_ktile(t):
            sl = t % 7
            P.dma("sp", kring[0][:, :, sl, :], NAKv[:, :, t * 128:(t + 1) * 128], writes=[kb[sl]], sembuf=kb[sl])
            P.dma("sp", vring[0][:, sl, :], scr["NAV"][t * 128:(t + 1) * 128, :], writes=[vb[sl]], sembuf=vb[sl])

        for t in range(29, 34):
            ld_ktile(t)

        def to_T(src, srcb, dstT, dstTb, s):
            pb, ps = P.bank()
            psb = ps.bitcast(BF16)
            for k in range(4):
                P.tr(psb[:, k * 128:(k + 1) * 128], src[:, k * 128:(k + 1) * 128], identb[0][:, :], [srcb, identb[1]], [pb])
            cnt["ev"] += 1
            P.copy("act" if cnt["ev"] % 2 else "dve", dstT[:, :, s * 128:(s + 1) * 128],
                   psb[:, 0:512].rearrange("p (k t) -> p k t", t=128), [pb], [dstTb])

        for m in range(7, -1, -1):
            t0 = m * 512
            P.dma("sp", x1t[0][:, :, :], X1[t0:t0 + 512, :].rearrange("(s p) d -> p s d", p=128), writes=[x1t[1]], sembuf=x1t[1])
            qsrc = scr["NAQT"].rearrange("(c hh d) t -> hh d c t", hh=2, d=64)
            for hh in range(2):
                P.dma("sp", qTm[hh][0][hh * 64:(hh + 1) * 64, :, :], qsrc[hh][:, :, t0:t0 + 512],
                      writes=[qTm[hh][1]], sembuf=qTm[hh][1])
            P.dma("sp", gqT[0][:, :, :], scr["GQT"].rearrange("(c p) t -> p c t", p=128)[:, :, t0:t0 + 512],
                  writes=[gqT[1]], sembuf=gqT[1])
            P.dma("sp", gkT[0][:, :, :], scr["GKT"].rearrange("(c p) t -> p c t", p=128)[:, :, t0:t0 + 512],
                  writes=[gkT[1]], sembuf=gkT[1])
            P.dma("sp", gv[0][:, :, :], scr["GV"][t0:t0 + 512, :].rearrange("(s p) d -> p s d", p=128), writes=[gv[1]], sembuf=gv[1])
            P.dma("sp", dl1[0][:, :], scr["DLB"][:, t0:t0 + 512], writes=[dl1[1]], sembuf=dl1[1])

            def out_cb(s, po, pso, t0=t0):
                i = cnt["g"] % 2
                cnt["g"] += 1
                of, ofb = ofs[i]
                gg, ggb = ggs[i]
                osum, osb = osums[i]
                ogl, oglb = ogls[i]
                ssq, ssqb = ssqs[i]
                r0 = t0 + s * 128
                P.dma("sp", of[:, :], scr["OF"][r0:r0 + 128, :], writes=[ofb], sembuf=ofb)
                P.dma("sp", gg[:, :], scr["GG"][r0:r0 + 128, :], writes=[ggb], sembuf=ggb)
                P.tt("dve", osum[:, :], pso[:, :], of[:, :], ALU.add, [po, ofb], [osb])
                for h in range(4):
                    P.act(junk[0][:, :], osum[:, h * 128:(h + 1) * 128], AF.Square, [osb], [junk[1], ssqb],
                          accum_out=ssq[:, h:h + 1])
                P.ts("pool", ssq[:, :], ssq[:, :], 1.0 / 128.0, 1e-6, ALU.mult, ALU.add, [ssqb], [ssqb])
                P.tt("pool", ssq[:, :], ssq[:, :], nh4[0][:, :], ALU.pow, [ssqb, nh4[1]], [ssqb])
                for h in range(4):
                    P.stt(osum[:, h * 128:(h + 1) * 128], osum[:, h * 128:(h + 1) * 128], ssq[:, h:h + 1], NG[0][:, :],
                          ALU.mult, ALU.mult, [osb, ssqb, NG[1]], [osb])
                P.act(gg[:, :], gg[:, :], AF.Silu, [ggb], [ggb])
                P.tt("pool", ogl[:, :], osum[:, :], gg[:, :], ALU.mult, [osb, ggb], [oglb])
                if "DBG_OGL" in scr:
                    P.dma("sp", scr["DBG_OGL"][r0:r0 + 128, :], ogl[:, :], reads=[oglb], sembuf=oglb)
                to_T(ogl, oglb, oglT[0], oglT[1], s)

            G.macro(1, gqT, gkT, gv, dl1, 512, [3, 2, 1, 0], out_cb)

            for s in range(3, -1, -1):
                j = m * 4 + s
                ks = min(max(j - 2, 0), 29)
                if j - 3 >= 0:
                    ld_ktile(j - 3)
                if j == 1:
                    P.dma("sp", tbl[0][:, :, :], prm["na_bias"][1], writes=[tbl[1]], sembuf=tbl[1])
                if j == 0:
                    P.dma("sp", tbl[0][:, :, :], prm["na_bias"][0], writes=[tbl[1]], sembuf=tbl[1])
                P.reserved = {6, 7}
                (po0, po1), po2 = P.bank2_fixed(6)
                pts = {}

                def scores(h):
                    c, hh = h // 2, h % 2
                    (pa, pb2), ps2 = P.bank2()
                    flat = ps2.rearrange("p a b -> p (a b)")
                    for kt in range(5):
                        sl = (ks + kt) % 7
                        P.mm(flat[:, kt * 128:(kt + 1) * 128], kring[0][:, c, sl, :],
                             qTm[hh][0][:, c, s * 128:(s + 1) * 128], True, True,
                             [kb[sl], qTm[hh][1]], [pa if kt < 4 else pb2])
                    i = cnt["i"] % 2
                    cnt["i"] += 1
                    Tt, Ttb = Tts[i]
                    Pt, Ptb = Pts[i]
                    P.stt(Tt[:, :], flat[:, 0:640], 0.125, tbl[0][:, h, :], ALU.mult, ALU.add, [pa, pb2, tbl[1]], [Ttb])
                    P.act(Pt[:, :, :].rearrange("p a b -> p (a b)"), Tt[:, :], AF.Exp, [Ttb], [Ptb])
                    pts[h] = (Pt, Ptb)

                def pv(h):
                    Pt, Ptb = pts[h]
                    hb, col0 = h // 4, (h % 4) * 65
                    pob = po0 if hb == 0 else po1
                    for kt in range(5):
                        sl = (ks + kt) % 7
                        P.mm(po2[:, hb, col0:col0 + 65], Pt[:, kt, :], vring[0][:, sl, h * 65:(h + 1) * 65],
                             kt == 0, kt == 4, [Ptb, vb[sl]], [pob])

                scores(0)
                for h in range(8):
                    if h + 1 < 8:
                        scores(h + 1)
                    pv(h)
                ii = s % 2
                ona, onab = onas[ii]
                rec, recb = recs[ii]
                for hb, pob in ((0, po0), (1, po1)):
                    P.op("dve", lambda e, hb=hb, rec=rec, po2=po2: e.reciprocal(
                        out=rec[:, hb * 4:(hb + 1) * 4],
                        in_=po2[:, hb, 0:260].rearrange("p (h e) -> p h e", e=65)[:, :, 64]),
                        reads=[pob], writes=[recb])
                for h in range(8):
                    hb, col0 = h // 4, (h % 4) * 65
                    pob = po0 if hb == 0 else po1
                    if h % 2 == 0:
                        P.ts("dve", ona[:, h * 64:(h + 1) * 64], po2[:, hb, col0:col0 + 64], rec[:, h:h + 1], None,
                             ALU.mult, None, [pob, recb], [onab])
                    else:
                        P.act(ona[:, h * 64:(h + 1) * 64], po2[:, hb, col0:col0 + 64], AF.Copy, [pob, recb], [onab],
                              scale=rec[:, h:h + 1])
                if "DBG_ONA" in scr:
                    P.dma("sp", scr["DBG_ONA"][j * 128:(j + 1) * 128, :], ona[:, :], reads=[onab], sembuf=onab)
                P.reserved = set()
                to_T(ona, onab, onaT[0], onaT[1], s)

            for dc in range(8):
                i = dc % 2
                gna, gnab = gnas[i]
                ggl, gglb = ggls[i]
                P.dma("sp", gna[:, :], scr["GNA"][dc * 128:(dc + 1) * 128, t0:t0 + 512], writes=[gnab], sembuf=gnab)
                P.dma("sp", ggl[:, :], scr["GGL"][dc * 128:(dc + 1) * 128, t0:t0 + 512], writes=[gglb], sembuf=gglb)
                pn, psn = P.bank()
                for k in range(4):
                    P.mm(psn[:, :], wbna[0][:, k, dc * 128:(dc + 1) * 128], onaT[0][:, k, :], k == 0, k == 3,
                         [wbna[1], onaT[1]], [pn])
                pg, psg = P.bank()
                for k in range(4):
                    P.mm(psg[:, :], wbgl[0][:, k, dc * 128:(dc + 1) * 128], oglT[0][:, k, :], k == 0, k == 3,
                         [wbgl[1], oglT[1]], [pg])
                P.act(gna[:, :], gna[:, :], AF.Sigmoid, [gnab], [gnab])
                P.act(ggl[:, :], ggl[:, :], AF.Sigmoid, [gglb], [gglb])
                P.tt("dve", gna[:, :], psn[:, :], gna[:, :], ALU.mult, [pn, gnab], [gnab])
                P.tt("dve", ggl[:, :], psg[:, :], ggl[:, :], ALU.mult, [pg, gglb], [gglb])
                P.tt("pool", mT[0][:, dc, :], gna[:, :], ggl[:, :], ALU.add, [gnab, gglb], [mT[1]])
            for s in range(4):
                for hf in range(2):
                    pb, ps = P.bank()
                    for k in range(8):
                        P.mm(ps[:, :], mT[0][:, k, s * 128:(s + 1) * 128], wout[0][:, k, hf * 512:(hf + 1) * 512],
                             k == 0, k == 7, [mT[1], wout[1]], [pb])
                    P.stt(x1t[0][:, s, hf * 512:(hf + 1) * 512], x1t[0][:, s, hf * 512:(hf + 1) * 512], ALPHA, ps[:, :],
                          ALU.mult, ALU.add, [pb, x1t[1]], [x1t[1]])
            for s in range(4):
                layer_norm_rows(P, C, x1t[0], x1t[1], s, Gt, Bt, cp)
            P.dma("sp", X2[t0:t0 + 512, :].rearrange("(s p) d -> p s d", p=128), x1t[0][:, :, :], reads=[x1t[1]], sembuf=x1t[1])
        P.barrier()
        P.emit()


WSHAPES = {"ffn1_w_gate": (D, DFF), "ffn1_w_up": (D, DFF), "ffn1_w_down": (DFF, D),
           "ffn2_w_gate": (D, DFF), "ffn2_w_up": (D, DFF), "ffn2_w_down": (DFF, D),
           "w_in": (D, N_IN), "w_branch_na": (512, D), "w_branch_gla": (512, D), "w_out": (D, D)}
VECS = ("ln1_g", "ln1_b", "ln2_g", "ln2_b", "ln3_g", "ln3_b")
SCR = {"NAQT": ([512, NOWN], BF16), "NAKT": ([512, NLOC], BF16), "NAV": ([NLOC, 520], BF16),
       "GQT": ([256, NOWN], F32), "GKT": ([256, NOWN], F32), "GV": ([NOWN, 512], BF16), "GG": ([NOWN, 512], F32),
       "DLB": ([16, NOWN], BF16), "GNA": ([D, NOWN], F32), "GGL": ([D, NOWN], F32), "OF": ([NOWN, 512], F32),
       "SIN": ([256, 128], F32), "X2": ([NOWN, D], F32)}


def build(debug=False, stop_after=99):
    nc = bass.Bass("TRN2", target_bir_lowering=False)
    x = nc.dram_tensor("x", [NLOC, D], F32, kind="ExternalInput").ap()
    ident_in = nc.dram_tensor("ident", [128, 128], F32, kind="ExternalInput").ap()
    W = {n: nc.dram_tensor(n, list(shp), F32, kind="ExternalInput").ap() for n, shp in WSHAPES.items()}
    vecs = {n: nc.dram_tensor(n, [1, D], F32, kind="ExternalInput").ap() for n in VECS}
    prm = {
        "na_bias": nc.dram_tensor("na_bias", [3, 128, 8, 640], F32, kind="ExternalInput").ap(),
        "gla_mask": nc.dram_tensor("gla_mask", [128, 2, 128], F32, kind="ExternalInput").ap(),
        "wdec": nc.dram_tensor("wdec", [16, 2, 256], F32, kind="ExternalInput").ap(),
        "bdec": nc.dram_tensor("bdec", [128, 4], F32, kind="ExternalInput").ap(),
        "norm_g": nc.dram_tensor("norm_g", [1, 128], F32, kind="ExternalInput").ap(),
        "sel": nc.dram_tensor("sel", [1, 8], F32, kind="ExternalInput").ap(),
    }
    out = nc.dram_tensor("out", [NOWN, D], F32, kind="ExternalOutput").ap()
    S = {n: nc.dram_tensor("s_" + n, list(shp), BF16).ap() for n, shp in WSHAPES.items()}
    dbg = {"kind": "ExternalOutput"} if debug else {}
    x1 = nc.dram_tensor("x1s", [NLOC, D], F32, **dbg).ap()
    scr = {}
    for n, (shp, dt) in SCR.items():
        kw = dbg if (debug and n in ("OF", "X2", "NAKT", "GQT", "SIN")) else {}
        scr[n] = nc.dram_tensor("scr_" + n, shp, dt, **kw).ap()
    if debug:
        scr["DBG_ONA"] = nc.dram_tensor("scr_DBG_ONA", [NOWN, 512], BF16, kind="ExternalOutput").ap()
        scr["DBG_OGL"] = nc.dram_tensor("scr_DBG_OGL", [NOWN, 512], BF16, kind="ExternalOutput").ap()
    sout_t = nc.dram_tensor("scr_SOUT", [256, 128], F32)
    sall_t = nc.dram_tensor("scr_SALL", [2048, 128], F32)
    scr["SOUT"] = sout_t.ap()
    scr["SALL"] = sall_t.ap()

    with ExitStack() as es:
        P = Prog(nc, es)
        block = es.enter_context(nc.Block())
        P.block = block
        sb = {n: Buf("S_" + n) for n in WSHAPES}
        first = ("ffn1_w_gate", "ffn1_w_up", "ffn1_w_down")
        later = ("w_in", "w_branch_na", "w_branch_gla", "w_out", "ffn2_w_gate", "ffn2_w_up", "ffn2_w_down")
        convert_weights(P, [(W[n], S[n], sb[n]) for n in first], ["act", "dve", "pool"], "sp")
        wb1 = {"wg": sb["ffn1_w_gate"], "wu": sb["ffn1_w_up"], "wd": sb["ffn1_w_down"], "ident": ident_in}
        ffn_phase(P, x, x1, NLOC, S["ffn1_w_gate"], S["ffn1_w_up"], S["ffn1_w_down"], wb1,
                  vecs["ln1_g"], vecs["ln1_b"], bg_items=[(W[n], S[n]) for n in later])
        src2 = x1[0:NOWN, :]
        if stop_after >= 2:
            proj_gla_phase(P, x1, S["w_in"], scr, prm, ident_in)
        if stop_after >= 3:
            ccs = P.newsem("cc")
            o = Op("pool", lambda e: e.collective_compute("AllGather", ALU.bypass, replica_groups=[list(range(8))],
                                                          ins=[sout_t.ap()], outs=[sall_t.ap()]))
            o.dma = (ccs, 1)
            o.cc = True
            P.ops["pool"].append(o)
            P.dma_events.append(o)
            P.barrier()
            P.emit()
            select_state_phase(P, scr, prm)
            if DBG.get("p3", 1):
                attn_phase(P, x1, scr["X2"], scr, prm, S, vecs["ln2_g"], vecs["ln2_b"], ident_in)
                src2 = scr["X2"]
        wb2 = {"wg": sb["ffn2_w_gate"], "wu": sb["ffn2_w_up"], "wd": sb["ffn2_w_down"], "ident": ident_in}
        ffn_phase(P, src2, out, NOWN, S["ffn2_w_gate"], S["ffn2_w_up"], S["ffn2_w_down"], wb2,
                  vecs["ln3_g"], vecs["ln3_b"])
    return nc


def local_token_index(core):
    if core % 2 == 0:
        return np.arange(0, NLOC)
    return 8191 - np.arange(0, NLOC)


def na_bias_tables(rpb, core):
    G = local_token_index(core)
    out = np.empty((3, 128, 8, 5, 128), dtype=np.float32)

    def table(j):
        ks = min(max(j - 2, 0), 29)
        q = G[j * 128:(j + 1) * 128]
        k = G[ks * 128:(ks + 5) * 128]
        qr, qc = q // 64, q % 64
        kr, kc = k // 64, k % 64
        rs = np.clip(qr - 4, 0, 120)
        cs = np.clip(qc - 8, 0, 48)
        valid = ((kr[:, None] >= rs[None, :]) & (kr[:, None] < rs[None, :] + 8) &
                 (kc[:, None] >= cs[None, :]) & (kc[:, None] < cs[None, :] + 16))
        assert (valid.sum(0) == 128).all(), "window not covered by key tiles"
        dr = np.clip(kr[:, None] - qr[None, :] + 7, 0, 14)
        dc = np.clip(kc[:, None] - qc[None, :] + 15, 0, 30)
        t = rpb[:, dr, dc]
        t = np.where(valid[None], t, np.float32(-30000.0)).astype(np.float32)
        return t.reshape(8, 5, 128, 128).transpose(2, 0, 1, 3)

    out[0] = table(0)
    out[1] = table(1)
    out[2] = table(2)
    for j in (3, 10, 30, 31):
        assert np.array_equal(table(j), out[2])
    return out.reshape(3, 128, 8, 640)


def make_in_maps(inputs):
    f = lambda n: np.asarray(inputs[n], dtype=np.float32)
    x = f("x")
    ident = np.eye(128, dtype=np.float32)
    rpb = f("na_rpb")[0]
    wdec = f("gla_w_dec2")[0]
    bdec = f("gla_b_dec")[0]
    w_in = f("w_in")[0]
    w_in_sw = w_in.copy()
    w_in_sw[:, 3072:3088] = w_in[:, 3088:3104]
    w_in_sw[:, 3088:3104] = w_in[:, 3072:3088]
    tri = np.arange(128)
    shared = {n: np.ascontiguousarray(f(n)[0]) for n in WSHAPES if n != "w_in"}
    for n in VECS:
        shared[n] = np.ascontiguousarray(f(n)[0].reshape(1, D))
    shared["norm_g"] = np.ascontiguousarray(f("gla_norm_g")[0].reshape(1, 128))
    tables = {0: na_bias_tables(rpb, 0), 1: na_bias_tables(rpb, 1)}
    maps = []
    for c in range(8):
        b, odd = c // 2, c % 2
        idx = local_token_index(c)
        m = dict(shared)
        m["x"] = np.ascontiguousarray(x[b][idx])
        m["ident"] = ident
        m["w_in"] = w_in_sw if odd else w_in
        m["na_bias"] = tables[odd]
        g0 = 1 if odd else 0
        m["wdec"] = np.ascontiguousarray(np.stack([wdec[g0], wdec[1 - g0]], axis=1))
        bd = np.stack([bdec[g0], bdec[1 - g0]], axis=0)
        m["bdec"] = np.ascontiguousarray(bd.reshape(2, 2, 128).transpose(2, 0, 1).reshape(128, 4))
        diag0 = (not odd)
        mk = np.zeros((128, 2, 128), dtype=np.float32)
        mk[:, 0, :] = (tri[:, None] < tri[None, :]) | ((tri[:, None] == tri[None, :]) & diag0)
        mk[:, 1, :] = (tri[:, None] > tri[None, :]) | ((tri[:, None] == tri[None, :]) & (not diag0))
        m["gla_mask"] = mk
        sel = np.zeros((1, 8), dtype=np.float32)
        sel[0, c ^ 1] = 1.0
        m["sel"] = sel
        maps.append(m)
    return maps


def assemble(results, key="out", n=NOWN):
    outp = np.empty((4, 8192, D), dtype=np.float32)
    for c in range(8):
        b = c // 2
        idx = local_token_index(c)[:n]
        outp[b][idx] = np.asarray(results[c][key])[:n]
    return outp


def kernel(**inputs):
    nc = build()
    in_maps = make_in_maps(inputs)
    res = run_bass_kernel_spmd(nc, in_maps, core_ids=list(range(8)))
    return assemble(res.results)
```
